# Optimizing a Trainium2 kernel written in Bass

```python
import math
import jax, jax.numpy as jnp
from jax import lax
import numpy as np

D_MODEL = 4096
BATCH = 4
SEQ = 4096
DEPTH = 4

GRID_W = 64
CTX_LEN = 256
N_MIXERS = 2
RET_HEADS = 16
RET_HEAD_DIM = D_MODEL // RET_HEADS
RET_CHUNK = 128
ROPE_THETA = 10000.0
POOL_WINDOWS = (2, 4, 8, 16)
POOL_GROUPS = len(POOL_WINDOWS)
POOL_GROUP_DIM = D_MODEL // POOL_GROUPS
D_FF = 11 * D_MODEL // 8
ADA_RANK = D_MODEL // 8
N_MOD = 9
EPS = 1e-6

kernel_name = "hybrid_retention_pool_macaron_dit"


def rms_norm(t, gain):
    tf = t.astype(jnp.float32)
    out = tf * lax.rsqrt(jnp.mean(tf * tf, axis=-1, keepdims=True) + EPS)
    return out.astype(t.dtype) * gain


def ada_modulation(cvec, w_down, w_up, b):
    m = (jax.nn.silu(cvec) @ w_down) @ w_up + b
    return m.reshape(cvec.shape[0], N_MOD, 1, D_MODEL)


def pre_mod(t, mods, k, gain):
    return rms_norm(t, gain) * (1.0 + mods[:, 3 * k + 1]) + mods[:, 3 * k]


def post_add(t, y, mods, k, gain, weight):
    return t + weight * mods[:, 3 * k + 2] * rms_norm(y, gain)


def swiglu(h, w_gate, w_up, w_down):
    return (jax.nn.silu(h @ w_gate) * (h @ w_up)) @ w_down


def axial_rope_tables(rows, cols):
    n_freq = RET_HEAD_DIM // 4
    inv = ROPE_THETA ** (-jnp.arange(n_freq, dtype=jnp.float32) / n_freq)
    ang = jnp.concatenate([rows[:, None].astype(jnp.float32) * inv,
                           cols[:, None].astype(jnp.float32) * inv], axis=-1)
    return jnp.cos(ang), jnp.sin(ang)


def apply_rope(t, cos, sin):
    t2 = t.reshape(*t.shape[:-1], RET_HEAD_DIM // 2, 2)
    a, b = t2[..., 0], t2[..., 1]
    return jnp.stack([a * cos - b * sin, a * sin + b * cos], axis=-1).reshape(t.shape)


def split_heads(t):
    b, l, _ = t.shape
    return t.reshape(b, l, RET_HEADS, RET_HEAD_DIM).transpose(0, 2, 1, 3).astype(jnp.float32)


def head_norm(o):
    mu = jnp.mean(o, axis=-1, keepdims=True)
    d = o - mu
    o = d * lax.rsqrt(jnp.mean(d * d, axis=-1, keepdims=True) + EPS)
    b, h, l, dh = o.shape
    return o.transpose(0, 2, 1, 3).reshape(b, l, h * dh)


def chunk_retention(q, k, v, log_g, s0):
    b, h, l, dh = q.shape
    nc = l // RET_CHUNK
    idx = jnp.arange(RET_CHUNK, dtype=jnp.float32)
    lg = log_g[:, None]
    rel = idx[:, None] - idx[None, :]
    inner = jnp.where(rel >= 0, jnp.exp(lg[..., None] * jnp.maximum(rel, 0.0)), 0.0)
    q_decay = jnp.exp(lg * (idx + 1.0))[:, :, None]
    k_decay = jnp.exp(lg * (RET_CHUNK - 1.0 - idx))[:, :, None]
    chunk_decay = jnp.exp(log_g * RET_CHUNK)[:, None, None]

    def to_chunks(t):
        return t.reshape(b, h, nc, RET_CHUNK, t.shape[-1]).transpose(2, 0, 1, 3, 4)

    def step(s, qkv):
        qc, kc, vc = qkv
        scores = jnp.einsum('bhid,bhjd->bhij', qc, kc) * inner
        o = (jnp.einsum('bhij,bhjv->bhiv', scores, vc)
             + jnp.einsum('bhid,bhdv->bhiv', qc * q_decay, s))
        s = chunk_decay * s + jnp.einsum('bhjd,bhjv->bhdv', kc * k_decay, vc)
        return s, o

    s_fin, o = lax.scan(step, s0, (to_chunks(q), to_chunks(k), to_chunks(v)))
    return o.transpose(1, 2, 0, 3, 4).reshape(b, h, l, dh), s_fin


def retention_final_state(k, v, log_g):
    l = k.shape[2]
    w = jnp.exp(log_g[:, None] * (l - 1.0 - jnp.arange(l, dtype=jnp.float32)))
    return jnp.einsum('bhld,hl,bhlv->bhdv', k, w, v)


def retention_mixer(hx, hc, w_in, w_out, decay_logit, rope_cos, rope_sin, ctx_out):
    d = D_MODEL
    log_g = jax.nn.log_sigmoid(decay_logit.astype(jnp.float32))
    scale = RET_HEAD_DIM ** -0.5
    qx, kx, vx, gfx, gbx = jnp.split(hx @ w_in, 5, axis=-1)
    n_cols = 5 if ctx_out else 3
    proj_c = hc @ w_in[:, :n_cols * d]
    qc, kc, vc = split_heads(proj_c[..., :d]) * scale, split_heads(proj_c[..., d:2 * d]), split_heads(proj_c[..., 2 * d:3 * d])
    qx = apply_rope(split_heads(qx), rope_cos, rope_sin) * scale
    kx = apply_rope(split_heads(kx), rope_cos, rope_sin)
    vx = split_heads(vx)
    flip = lambda t: jnp.flip(t, axis=2)
    s_zero = jnp.zeros((hc.shape[0], RET_HEADS, RET_HEAD_DIM, RET_HEAD_DIM), jnp.float32)
    if ctx_out:
        oc_f, s_f = chunk_retention(qc, kc, vc, log_g[0], s_zero)
        oc_b, s_b = chunk_retention(flip(qc), flip(kc), flip(vc), log_g[1], s_zero)
        gfc, gbc = proj_c[..., 3 * d:4 * d], proj_c[..., 4 * d:]
        yc = (head_norm(oc_f) * jax.nn.silu(gfc) + head_norm(flip(oc_b)) * jax.nn.silu(gbc)).astype(hc.dtype) @ w_out
    else:
        s_f = retention_final_state(kc, vc, log_g[0])
        s_b = retention_final_state(flip(kc), flip(vc), log_g[1])
        yc = None
    ox_f, _ = chunk_retention(qx, kx, vx, log_g[0], s_f)
    ox_b, _ = chunk_retention(flip(qx), flip(kx), flip(vx), log_g[1], s_b)
    yx = (head_norm(ox_f) * jax.nn.silu(gfx) + head_norm(flip(ox_b)) * jax.nn.silu(gbx)).astype(hx.dtype) @ w_out
    return yx, yc


def centred_box_sum(t, axis, w):
    n = t.shape[axis]
    pad = [(0, 0)] * t.ndim
    pad[axis] = (1, 0)
    cs = jnp.pad(jnp.cumsum(t, axis=axis), pad)
    i = jnp.arange(n)
    lo = jnp.clip(i - w // 2, 0, n)
    hi = jnp.clip(i + w // 2, 0, n)
    return jnp.take(cs, hi, axis=axis) - jnp.take(cs, lo, axis=axis), (hi - lo)


def multi_scale_pool(h, w_pool, scale, grid_rows):
    b, l, d = h.shape
    hf = h.astype(jnp.float32)
    if grid_rows is None:
        hf = hf.reshape(b, l, POOL_GROUPS, POOL_GROUP_DIM)
    else:
        hf = hf.reshape(b, grid_rows, GRID_W, POOL_GROUPS, POOL_GROUP_DIM)
    outs = []
    for g, w in enumerate(POOL_WINDOWS):
        t = hf[..., g, :]
        if grid_rows is None:
            s, cnt = centred_box_sum(t, 1, w)
            mean = s / cnt.astype(jnp.float32)[None, :, None]
        else:
            s, cr = centred_box_sum(t, 1, w)
            s, cc = centred_box_sum(s, 2, w)
            cnt = (cr[:, None] * cc[None, :]).astype(jnp.float32)
            mean = s / cnt[None, :, :, None]
        outs.append(mean - t)
    p = jnp.stack(outs, axis=-2).reshape(b, l, POOL_GROUPS, POOL_GROUP_DIM).astype(h.dtype)
    y = jnp.einsum('blgc,gce->blge', p, w_pool).reshape(b, l, d)
    return y * scale


def setup_inputs(seed: int = 0) -> dict:
    key = jax.random.key(seed)
    ks = jax.random.split(key, 16)
    d = D_MODEL
    n_ret = len(range(0, DEPTH, N_MIXERS))
    n_pool = len(range(1, DEPTH, N_MIXERS))
    nrm = jax.random.normal
    hidx = jnp.arange(RET_HEADS, dtype=jnp.float32)
    base_logit = jnp.log1p(-jnp.exp2(-5.0 - hidx)) + (5.0 + hidx) * math.log(2.0)
    return {
        "x": nrm(ks[0], (BATCH, SEQ, d), jnp.float32),
        "c": nrm(ks[1], (BATCH, d), jnp.float32),
        "ctx": nrm(ks[2], (BATCH, CTX_LEN, d), jnp.float32),
        "c_ctx": nrm(ks[3], (d,), jnp.float32),
        "w_ada_down": nrm(ks[4], (DEPTH, d, ADA_RANK), jnp.float32) * d ** -0.5,
        "w_ada_up": nrm(ks[5], (DEPTH, ADA_RANK, N_MOD * d), jnp.float32) * (0.8 * ADA_RANK ** -0.5),
        "b_ada": 0.02 * nrm(ks[6], (DEPTH, N_MOD * d), jnp.float32),
        "norm_gain": 1.0 + 0.05 * nrm(ks[7], (DEPTH, 6, d), jnp.float32),
        "w_ffn_gate": nrm(ks[8], (DEPTH, 2, d, D_FF), jnp.float32) * d ** -0.5,
        "w_ffn_up": nrm(ks[9], (DEPTH, 2, d, D_FF), jnp.float32) * d ** -0.5,
        "w_ffn_down": nrm(ks[10], (DEPTH, 2, D_FF, d), jnp.float32) * D_FF ** -0.5,
        "w_ret_in": nrm(ks[11], (n_ret, d, 5 * d), jnp.float32) * d ** -0.5,
        "w_ret_out": nrm(ks[12], (n_ret, d, d), jnp.float32) * d ** -0.5,
        "ret_decay_logit": base_logit[None, None, :] + 0.1 * nrm(ks[13], (n_ret, 2, RET_HEADS), jnp.float32),
        "w_pool": nrm(ks[14], (n_pool, POOL_GROUPS, POOL_GROUP_DIM, POOL_GROUP_DIM), jnp.float32) * POOL_GROUP_DIM ** -0.5,
        "pool_scale": 1.0 + 0.1 * nrm(ks[15], (n_pool, d), jnp.float32),
    }


def reference(x, c, ctx, c_ctx, w_ada_down, w_ada_up, b_ada, norm_gain, w_ffn_gate, w_ffn_up,
              w_ffn_down, w_ret_in, w_ret_out, ret_decay_logit, w_pool, pool_scale):
    seq = x.shape[1]
    rows = seq // GRID_W
    pos = jnp.arange(seq)
    rope_cos, rope_sin = axial_rope_tables(pos // GRID_W, pos % GRID_W)
    last_ret = ((DEPTH - 1) // N_MIXERS) * N_MIXERS
    for i in range(DEPTH):
        kind = i % N_MIXERS
        j = i // N_MIXERS
        ctx_in = i <= last_ret
        ctx_out = i < last_ret
        g = norm_gain[i]
        mx = ada_modulation(c, w_ada_down[i], w_ada_up[i], b_ada[i])
        x = post_add(x, swiglu(pre_mod(x, mx, 0, g[0]), w_ffn_gate[i, 0], w_ffn_up[i, 0], w_ffn_down[i, 0]),
                     mx, 0, g[1], 0.5)
        if ctx_in:
            mc = ada_modulation(c_ctx[None, :], w_ada_down[i], w_ada_up[i], b_ada[i])
            ctx = post_add(ctx, swiglu(pre_mod(ctx, mc, 0, g[0]), w_ffn_gate[i, 0], w_ffn_up[i, 0], w_ffn_down[i, 0]),
                           mc, 0, g[1], 0.5)
        hx = pre_mod(x, mx, 1, g[2])
        if kind == 0:
            hc = pre_mod(ctx, mc, 1, g[2])
            yx, yc = retention_mixer(hx, hc, w_ret_in[j], w_ret_out[j], ret_decay_logit[j],
                                     rope_cos, rope_sin, ctx_out)
        else:
            yx = multi_scale_pool(hx, w_pool[j], pool_scale[j], rows)
            yc = multi_scale_pool(pre_mod(ctx, mc, 1, g[2]), w_pool[j], pool_scale[j], None) if ctx_out else None
        x = post_add(x, yx, mx, 1, g[3], 1.0)
        x = post_add(x, swiglu(pre_mod(x, mx, 2, g[4]), w_ffn_gate[i, 1], w_ffn_up[i, 1], w_ffn_down[i, 1]),
                     mx, 2, g[5], 0.5)
        if ctx_out:
            ctx = post_add(ctx, yc, mc, 1, g[3], 1.0)
            ctx = post_add(ctx, swiglu(pre_mod(ctx, mc, 2, g[4]), w_ffn_gate[i, 1], w_ffn_up[i, 1], w_ffn_down[i, 1]),
                           mc, 2, g[5], 0.5)
    return x
```

```python
import math
from contextlib import ExitStack
import numpy as np
import concourse.bass as bass
import concourse.mybir as mybir
from concourse.bass_utils import run_bass_kernel_spmd

F32, BF16 = mybir.dt.float32, mybir.dt.bfloat16
AF = mybir.ActivationFunctionType
ALU = mybir.AluOpType
EPS = 1e-6
P = 128


class Cfg:
    def __init__(self, D=4096, S=4096, B=4, CTX=256, DEPTH=4, GW=64):
        self.D, self.S, self.B, self.CTX, self.DEPTH, self.GW = D, S, B, CTX, DEPTH, GW
        self.H = D // 256
        self.FF = 11 * D // 8
        self.R = D // 8
        self.NC = 2 * B
        self.DC, self.FC, self.RC = D // P, self.FF // P, self.R // P
        self.NT = S // 2
        self.NCH = self.NT // P
        self.ROWS = self.NT // GW
        self.GDC = self.DC // 4
        self.NRET = len(range(0, DEPTH, 2))
        self.NPOOL = len(range(1, DEPTH, 2))
        self.last_ret = ((DEPTH - 1) // 2) * 2


def chunk_rows(rows, n):
    lim = max(1, 262144 // n)
    return max(d_ for d_ in range(1, rows + 1) if rows % d_ == 0 and d_ <= lim)


class Op:
    __slots__ = ("stream", "kind", "fn", "sem", "cdeps", "ddeps", "inc", "val", "idx")


class Prog:
    CST = ("pe", "act", "dve", "pool")

    def __init__(self):
        self.ops = []
        self.lastw = {}
        self.readers = {}

    def add(self, stream, fn, reads=(), writes=(), kind="c", sem=None):
        op = Op()
        op.stream, op.kind, op.fn, op.sem = stream, kind, fn, sem
        idx = len(self.ops)
        op.idx = idx
        deps = set()
        lw, rd = self.lastw, self.readers
        for k in reads:
            w = lw.get(k)
            if w is not None:
                deps.add(w)
        for k in writes:
            w = lw.get(k)
            if w is not None:
                deps.add(w)
            r = rd.get(k)
            if r:
                deps.update(r)
        for k in reads:
            rd.setdefault(k, []).append(idx)
        for k in writes:
            lw[k] = idx
            rd[k] = []
        deps.discard(idx)
        op.cdeps = deps
        op.inc = False
        self.ops.append(op)
        return op

    def resolve(self):
        ops = self.ops
        known_c = {s: {t: -1 for t in self.CST} for s in ("pe", "act", "dve", "pool", "sp")}
        known_d = {s: {} for s in known_c}
        gcount = {}
        waits = [None] * len(ops)
        for op in ops:
            w = []
            E = op.stream
            best = {}
            dd = {}
            for j in op.cdeps:
                pj = ops[j]
                if pj.kind == "c":
                    if pj.stream == E and E == "pe":
                        continue
                    if j > best.get(pj.stream, -1):
                        best[pj.stream] = j
                else:
                    dd[pj.sem] = pj.kind
            for X, j in best.items():
                if j > known_c[E][X]:
                    known_c[E][X] = j
                    ops[j].inc = True
                    w.append(("c", j))
            for g, kd in dd.items():
                val = gcount[g] * (16 if kd == "d" else 1)
                if val > known_d[E].get(g, 0):
                    known_d[E][g] = val
                    w.append(("d", g, val))
            waits[op.idx] = w
            if op.kind != "c":
                gcount[op.sem] = gcount.get(op.sem, 0) + 1
        cnt = {s: 0 for s in self.CST}
        for op in ops:
            if op.kind == "c" and op.inc:
                cnt[op.stream] += 1
                op.val = cnt[op.stream]
        self.waits = waits
        self.cnt = cnt
        self.groups = sorted(gcount.keys())


EPOCH = 30000
PUMP = 24


import os as _os
NOCC = bool(_os.environ.get("MK_NOCC"))
NOXCH = bool(_os.environ.get("MK_NOXCH"))


class Builder:
    def __init__(self, cfg, stop_after=None):
        self.c = cfg
        self.WNC = cfg.NC
        self.nc = bass.Bass("TRN2", target_bir_lowering=False)
        self.p = Prog()
        self.stop_after = stop_after
        self.bars = []

    def op(self, stream, fn, R=(), W=(), kind="c", sem=None):
        o = self.p.add(stream, fn, R, W, kind, sem)
        if self._pending_bar.get(stream) is not None:
            o.cdeps.update(self._pending_bar.pop(stream))
        return o

    def barrier(self, final=False, keep=False):
        last = {}
        for op in self.p.ops[self._bar_start:]:
            key = op.stream if op.kind == "c" else ("g", op.sem)
            last[key] = op.idx
        for k, v in self._bar_prev.items():
            last.setdefault(k, v)
        self._bar_prev = last
        self._bar_start = len(self.p.ops)
        deps = set(v for k, v in last.items()
                   if final or not (isinstance(k, tuple) and k[1].startswith("bg")))
        self._pending_bar = {s: set(deps) for s in ("pe", "act", "dve", "pool", "sp")}
        keep_ = {k: v for k, v in self.p.lastw.items() if isinstance(k, tuple) and k[0] in ("W", "wsh", "wmid")}
        if not keep:
            self.p.lastw = keep_
            self.p.readers = {}
            self.ar = self.pers_end

    def alloc(self, n32, dt=F32, shape=None):
        a = self.A[:, self.ar:self.ar + n32]
        self.ar += n32
        assert self.ar <= self.NA, ("arena overflow", self.ar, self.NA)
        if dt == BF16:
            a = a.bitcast(BF16)
        return a

    def mm(self, out, lhsT, rhs, start, stop, R, W):
        self.op("pe", lambda e: e.matmul(out, lhsT, rhs, start=start, stop=stop), R, W)

    def tr(self, out, in_, R, W):
        idn = self.ident
        self.op("pe", lambda e: e.transpose(out, in_, idn), tuple(R) + ("ident",), W)

    def act(self, out, in_, func, R, W, bias=None, scale=None):
        kw = {}
        if bias is not None:
            kw["bias"] = bias
        if scale is not None:
            kw["scale"] = scale
        self.op("act", lambda e: e.activation(out, in_, func, **kw), R, W)

    def tt(self, eng, out, in0, in1, op, R, W):
        self.op(eng, lambda e: e.tensor_tensor(out, in0, in1, op), R, W)

    def ts(self, eng, out, in0, s1, s2, op0, op1, R, W):
        if op1 is None:
            self.op(eng, lambda e: e.tensor_scalar(out, in0, s1, None, op0), R, W)
        else:
            self.op(eng, lambda e: e.tensor_scalar(out, in0, s1, s2, op0, op1), R, W)

    def stt(self, out, in0, scalar, in1, op0, op1, R, W):
        self.op("dve", lambda e: e.scalar_tensor_tensor(out, in0, scalar, in1, op0, op1), R, W)

    def cp(self, eng, out, in_, R, W):
        if eng == "act":
            self.op("act", lambda e: e.copy(out, in_), R, W)
        else:
            self.op(eng, lambda e: e.tensor_copy(out, in_), R, W)

    def ld(self, out, in_, R, W, sem):
        self.op("sp", lambda e: e.dma_start(out=out, in_=in_), R, W, "d", sem)

    def st(self, out, in_, R, W, sem):
        self.op("pool", lambda e: e.dma_start(out=out, in_=in_), R, W, "d", sem)

    def din(self, name, shape, dt=F32):
        self.in_names.append(name)
        return self.nc.dram_tensor(name, list(shape), dt, kind="ExternalInput").ap()

    def dscr(self, name, shape, dt):
        return self.nc.dram_tensor(name, list(shape), dt).ap()

    def declare(self):
        c = self.c
        nc = self.nc
        self.in_names = []
        D, NT, CTX, NC, FF, R, H = c.D, c.NT, c.CTX, self.WNC, c.FF, c.R, c.H
        self.xT = self.din("xT", [D, NT])
        self.ctxT = self.din("ctxT", [D, CTX])
        self.cvT = self.din("cvT", [P, c.DC * 2])
        self.bada = self.din("bada", [c.DEPTH, P, 9 * c.DC])
        self.gain = self.din("gain", [c.DEPTH, P, 6 * c.DC])
        self.pscale = self.din("pscale", [c.NPOOL, P, c.DC])
        self.dlog = self.din("dlog", [c.NRET, 2 * H])
        self.rope = self.din("rope", [c.NCH, P, 4 * 128])
        self.icx = self.din("icx", [4, NT])
        self.icc = self.din("icc", [4, CTX])
        self.cst = self.din("cst", [P, 7 * 128 + 4])
        self.w_in = {}
        self.w_in["adn"] = self.din("w_adn", [c.DEPTH, D // NC, R])
        self.w_in["aup"] = self.din("w_aup", [c.DEPTH, R // NC, 9 * D])
        self.w_in["fg"] = self.din("w_fg", [c.DEPTH, 2, D // NC, FF])
        self.w_in["fu"] = self.din("w_fu", [c.DEPTH, 2, D // NC, FF])
        self.w_in["fd"] = self.din("w_fd", [c.DEPTH, 2, FF // NC, D])
        self.w_in["ri"] = self.din("w_ri", [c.NRET, D // NC, 5 * D])
        self.w_in["ro"] = self.din("w_ro", [c.NRET, D // NC, D])
        self.w_in["pl"] = self.din("w_pl", [c.NPOOL, D // NC, D // 4])
        self.outT = nc.dram_tensor("outT", [D, NT], F32, kind="ExternalOutput").ap()
        self.XS = self.dscr("XS", [D, NT], F32)
        self.CS = self.dscr("CSs", [D, CTX], F32)
        NTC = NT + CTX
        self.QS = self.dscr("QS", [NTC, D], BF16)
        self.KS = self.dscr("KS", [NTC, D], BF16)
        self.VS = self.dscr("VS", [NTC, D], BF16)
        self.GS = [self.dscr("GS%d" % i, [NTC, D], BF16) for i in range(2)]
        self.YFS = self.dscr("YFS", [NTC, D], BF16)
        self.YTS = self.dscr("YTS", [D, NTC], BF16)
        self.YBS = self.dscr("YBS", [NTC, D], BF16)
        self.HXP = self.dscr("HXP", [D, NTC], F32)
        self.PSC = self.dscr("PSC", [D, NTC], BF16)
        self.HSND = self.dscr("HSND", [2 * D, 512], F32)
        self.HRCV = self.dscr("HRCV", [4 * D, 512], F32)
        self.SND = self.dscr("SND", [2 * P, H * 512], F32)
        self.RCV = self.dscr("RCV", [4 * P, H * 512], F32)
        self.CST = self.dscr("CSTs", [2 * P, H * 512], F32)
        self.wfull = {}

    def prep_weight(self, key, src2d, lay):
        c = self.c
        rows, n = src2d.shape
        name = "W_" + "_".join(str(k) for k in key)
        full = self.dscr(name, [rows * c.NC, n], BF16)
        shard = self.dscr(name + "_s", [rows, n], BF16)
        self.wfull[key] = full
        bl = math.gcd(n, 2048)
        step = max(1, 8000 // (n // bl))
        for r0 in range(0, rows, step):
            r1 = min(rows, r0 + step)
            o_ap = shard[r0:r1, :].rearrange("r (a b) -> r a b", b=bl)
            i_ap = src2d[r0:r1, :].rearrange("r (a b) -> r a b", b=bl)
            self.bg.append(("pool", lambda e, o=o_ap, i=i_ap: e.dma_start(out=o, in_=i, max_dma_last_dim=bl * 4),
                            (), (("wsh", key),), "d", "bgcast_%s" % lay))
        rc = chunk_rows(rows, n)
        g4 = [list(range(i, i + 4)) for i in range(0, c.NC, 4)]
        g2 = [[i, i + 4] for i in range(4)]
        for ci in range(rows // rc):
            mid = self.dscr("%s_m%d" % (name, ci), [4 * rc, n], BF16)
            sh_c = shard[ci * rc:(ci + 1) * rc, :]
            fu_c = full[ci * 8 * rc:(ci + 1) * 8 * rc, :]
            self.bg.append(("pool", lambda e, s_=sh_c, f_=mid: e.collective_compute(
                "AllGather", ALU.bypass, replica_groups=g4, ins=[s_.opt()], outs=[f_.opt()]),
                (("wsh", key),), (("wmid", key, ci),), "cc", "bgag4_%s" % lay))
            self.bg.append(("pool", lambda e, s_=mid, f_=fu_c: e.collective_compute(
                "AllGather", ALU.bypass, replica_groups=g2, ins=[s_.opt()], outs=[f_.opt()]),
                (("wmid", key, ci),), (("W", key),), "cc", "bgag2_%s" % lay))

    def pump(self, n=None):
        k = len(self.bg) if n is None else min(n, len(self.bg))
        for _ in range(k):
            st_, fn, R, W, kind, sem = self.bg.pop(0)
            self.op(st_, fn, R, W, kind, sem)

    def prep_layer(self, l):
        c = self.c
        if l >= c.DEPTH:
            return
        lay = "L%d" % l
        for hf in range(2):
            self.prep_weight(("fg", l, hf), self.w_in["fg"][l, hf], lay)
            self.prep_weight(("fu", l, hf), self.w_in["fu"][l, hf], lay)
            self.prep_weight(("fd", l, hf), self.w_in["fd"][l, hf], lay)
        if l % 2 == 0:
            self.prep_weight(("ri", l // 2), self.w_in["ri"][l // 2], lay)
            self.prep_weight(("ro", l // 2), self.w_in["ro"][l // 2], lay)
        else:
            self.prep_weight(("pl", l // 2), self.w_in["pl"][l // 2], lay)

    def setup_persistent(self):
        c = self.c
        self.ar = 0
        self.cstt = self.alloc(7 * 128 + 4)
        k = self.cstt
        self.relf, self.relb = k[:, 0:128], k[:, 128:256]
        self.maskf, self.maskb = k[:, 256:384], k[:, 384:512]
        self.iota1, self.iotab = k[:, 512:640], k[:, 640:768]
        identf = k[:, 768:896]
        self.pj = k[:, 896:898]
        self.fl = k[:, 898:900]
        self.ident = self.alloc(64, BF16)
        self.ones = self.alloc(128)
        self.epsc = self.alloc(1)
        self.mods = self.alloc(c.DEPTH * 2 * 9 * c.DC)
        self.pers_end = self.ar
        self.ld(self.cstt, self.cst, (), ("cstt",), "ld_cst")
        self.cp("dve", self.ident, identf, ("cstt",), ("ident",))
        self.op("dve", lambda e: e.memset(self.ones, 1.0), (), ("ones",))
        self.op("dve", lambda e: e.memset(self.epsc, EPS), (), ("ones",))

    def mod(self, l, v, s):
        c = self.c
        o = ((l * 2 + v) * 9 + s) * c.DC
        return self.mods[:, o:o + c.DC]

    def phase_mods(self):
        c = self.c
        DC, RC, D, R = c.DC, c.RC, c.D, c.R
        cv = self.alloc(DC * 2)
        cs = self.alloc(DC, BF16)
        w1 = self.alloc(DC * R // 2, BF16)
        w2 = [self.alloc(RC * D // 2, BF16) for _ in range(2)]
        tb = self.alloc(RC, BF16)
        raw = self.alloc(9 * DC * 2)
        ba = self.alloc(9 * DC)
        gn = self.alloc(6 * DC)
        tmp = self.alloc(DC)
        self.ld(cv, self.cvT, (), ("cv",), "ld_m0")
        sg = self.alloc(DC * 2)
        self.act(sg, cv, AF.Sigmoid, ("cv",), ("sg",))
        self.tt("dve", cs, cv, sg, ALU.mult, ("cv", "sg"), ("cs",))
        cs3 = cs.rearrange("p (c v) -> p c v", v=2)
        tb3 = tb.rearrange("p (c v) -> p c v", v=2)
        raw4 = raw.rearrange("p (s c v) -> p s c v", s=9, v=2)
        for l in range(c.DEPTH):
            W1 = self.wfull[("adn", l)].rearrange("(c p) n -> p c n", p=P)
            self.ld(w1.rearrange("p (c n) -> p c n", n=R), W1, (("W", ("adn", l)),), ("w1",), "ld_m1")
            w13 = w1.rearrange("p (c n) -> p c n", n=R)
            ps = self.PS[0][:, 0:RC * 2].rearrange("p (c v) -> p c v", v=2)
            for rc in range(RC):
                for kc in range(DC):
                    self.mm(ps[:, rc, :], w13[:, kc, rc * P:(rc + 1) * P], cs3[:, kc, :], kc == 0, kc == DC - 1,
                            ("w1", "cs"), (("ps", 0),))
            self.cp("act", tb, self.PS[0][:, 0:RC * 2], (("ps", 0),), ("tb",))
            self.ld(ba, self.bada[l], (), ("ba",), "ld_m2")
            self.ld(gn, self.gain[l], (), ("gn",), "ld_m2")
            W2 = self.wfull[("aup", l)].rearrange("(c p) n -> p c n", p=P)
            for s in range(9):
                wb = w2[s % 2]
                wb3 = wb.rearrange("p (c n) -> p c n", n=D)
                self.ld(wb3, W2[:, :, s * D:(s + 1) * D], (("W", ("aup", l)),), (("w2", s % 2),), "ld_m3%d" % (s % 2))
                pb = 1 + s % 2
                ps2 = self.PS[pb][:, 0:DC * 2].rearrange("p (c v) -> p c v", v=2)
                for oc in range(DC):
                    for rc in range(RC):
                        self.mm(ps2[:, oc, :], wb3[:, rc, oc * P:(oc + 1) * P], tb3[:, rc, :], rc == 0, rc == RC - 1,
                                (("w2", s % 2), "tb"), (("ps", pb),))
                bav = ba[:, s * DC:(s + 1) * DC]
                for v in range(2):
                    self.tt("dve", raw4[:, s, :, v], ps2[:, :, v], bav, ALU.add, (("ps", pb), "ba"), (("raw", s, v),))
            for v in range(2):
                for k in range(3):
                    sh, sc, gt = raw4[:, 3 * k, :, v], raw4[:, 3 * k + 1, :, v], raw4[:, 3 * k + 2, :, v]
                    g0, g1 = gn[:, (2 * k) * DC:(2 * k + 1) * DC], gn[:, (2 * k + 1) * DC:(2 * k + 2) * DC]
                    rk = [("raw", 3 * k + i, v) for i in range(3)]
                    self.stt(self.mod(l, v, k), sc, 1.0, g0, ALU.add, ALU.mult, rk + ["gn"], (("mod", l, v, k),))
                    self.cp("dve", self.mod(l, v, 3 + k), sh, rk, (("mod", l, v, 3 + k),))
                    wk = 1.0 if k == 1 else 0.5
                    self.stt(self.mod(l, v, 6 + k), gt, wk, g1, ALU.mult, ALU.mult, rk + ["gn"], (("mod", l, v, 6 + k),))

    def rstd_from_ps(self, rs, psb, T, tag):
        inv = 1.0 / self.c.D
        pst = self.PS[psb][:, 0:T]
        self.act(rs, pst, AF.Sqrt, (("ps", psb),), (tag,), bias=self.epsc, scale=inv)
        self.op("dve", lambda e: e.reciprocal(rs, rs), (tag,), (tag,))

    def scr(self):
        i = self._scr_i % len(self.SCR)
        self._scr_i += 1
        return self.SCR[i], ("scr", i)

    def alloc_common(self, T, need_h=True):
        c = self.c
        self.BIG = self.alloc(c.DC * T).rearrange("p (c t) -> p c t", t=T)
        self.XN = self.alloc(c.DC * T // 2, BF16).rearrange("p (c t) -> p c t", t=T)
        if need_h:
            self.HB = self.alloc(c.FC * T // 2, BF16).rearrange("p (c t) -> p c t", t=T)
        self.WB = [self.alloc(2048, BF16) for _ in range(4)]
        self.RS = [self.alloc(T) for _ in range(2)]
        self.SCR = [self.alloc(T) for _ in range(6)]
        self._scr_i = 0
        self._wb_i = 0
        self._xn_keys = None

    def wslot(self):
        i = self._wb_i % 4
        self._wb_i += 1
        return self.WB[i], ("wb", i), "ld_wb%d" % i

    def premod(self, src, t0, T, l, v, k, to_f32=False):
        c = self.c
        DC = c.DC
        s3 = src.rearrange("(c p) t -> p c t", p=P)
        q = max(1, DC // 4)
        for i in range(0, DC, q):
            self.ld(self.BIG[:, i:i + q, :], s3[:, i:i + q, t0:t0 + T], (("x", id(src)),),
                    [("big", d) for d in range(i, i + q)], "ld_big%d" % (i // q))
        for dc in range(DC):
            sq, sk = self.scr()
            self.act(sq, self.BIG[:, dc, :], AF.Square, (("big", dc),), (sk,))
            self.mm(self.PS[7][:, 0:T], self.ones, sq, dc == 0, dc == DC - 1, (sk, "ones"), (("ps", 7),))
        rs = self.RS[0]
        self.rstd_from_ps(rs, 7, T, "rs0")
        A, SH = self.mod(l, v, k), self.mod(l, v, 3 + k)
        for dc in range(DC):
            tm, tk = self.scr()
            self.stt(tm, self.BIG[:, dc, :], A[:, dc:dc + 1], rs, ALU.mult, ALU.mult,
                     (("big", dc), "rs0", ("mod", l, v, k)), (tk,))
            if to_f32:
                self.act(self.BIG[:, dc, :], tm, AF.Identity, (tk, ("mod", l, v, 3 + k)), (("big", dc),),
                         bias=SH[:, dc:dc + 1], scale=1.0)
            else:
                self.act(self.XN[:, dc, :], tm, AF.Identity, (tk, ("mod", l, v, 3 + k)), (("xn", dc),),
                         bias=SH[:, dc:dc + 1], scale=1.0)

    def pf(self, mats, kcs_of, act_of, NOC, T, evac, banks_per=None):
        nm = len(mats(0))
        nb = 2 * nm
        blk = 0
        for ob in range(0, NOC, 2):
            nj = min(2, NOC - ob)
            ml = mats(ob)
            kcs = kcs_of(ob)
            nk = len(kcs)
            ksp = max(d_ for d_ in range(1, 17) if nk % d_ == 0)
            base = (blk % 2) * 4 if nm == 2 else (blk % 3) * 2
            blk += 1
            for k0 in range(0, nk, ksp):
                tiles = []
                for (Wap, col0, wkey) in ml:
                    wt, wk, wsem = self.wslot()
                    w3 = wt[:, 0:ksp * 128 * nj].rearrange("p (k n) -> p k n", n=128 * nj)
                    Wv = Wap.rearrange("(k p) n -> p k n", p=P)
                    kr0 = kcs[k0][0]
                    assert [kk[0] for kk in kcs[k0:k0 + ksp]] == list(range(kr0, kr0 + ksp))
                    self.ld(w3, Wv[:, kr0:kr0 + ksp, col0:col0 + 128 * nj], (("W", wkey),), (wk,), wsem)
                    tiles.append((w3, wk))
                for mi, (w3, wk) in enumerate(tiles):
                    for j in range(nj):
                        pb = base + mi * 2 + j
                        for ki in range(ksp):
                            a_ap, a_key = act_of(kcs[k0 + ki][1])
                            xk_ = getattr(self, "_xn_keys", None)
                            rkeys = (wk, a_key) if not (xk_ and a_key[0] == "xn") else tuple([wk] + xk_[a_key[1]])
                            self.mm(self.PS[pb][:, 0:T], w3[:, ki, j * 128:(j + 1) * 128], a_ap,
                                    (k0 + ki) == 0, (k0 + ki) == nk - 1, rkeys, (("ps", pb),))
            for j in range(nj):
                evac(ob + j, [base + mi * 2 + j for mi in range(nm)])

    def post_add(self, src, dst, t0, T, l, v, k):
        c = self.c
        DC = c.DC
        rs = self.RS[1]
        self.rstd_from_ps(rs, 6, T, "rs1")
        GG = self.mod(l, v, 6 + k)
        s3 = src.rearrange("(c p) t -> p c t", p=P)
        d3 = dst.rearrange("(c p) t -> p c t", p=P)
        for dc in range(DC):
            xc, xk = self.scr()
            self.ld(xc, s3[:, dc, t0:t0 + T], (("x", id(src)),), (xk,), "ld_" + xk[0] + str(xk[1]))
            tm, tk = self.scr()
            self.stt(tm, self.BIG[:, dc, :], GG[:, dc:dc + 1], rs, ALU.mult, ALU.mult,
                     (("big", dc), "rs1", ("mod", l, v, 6 + k)), (tk,))
            self.tt("dve", xc, xc, tm, ALU.add, (xk, tk), (xk,))
            self.st(d3[:, dc, t0:t0 + T], xc, (xk,), (("x", id(dst)),), "st_x")

    def y_evac(self, T, DC, scale_ap=None):
        def ev(dc, pbs):
            pb = pbs[0]
            pst = self.PS[pb][:, 0:T]
            sq, sk = self.scr()
            if scale_ap is None:
                self.cp("act", self.BIG[:, dc, :], pst, (("ps", pb),), (("big", dc),))
                self.act(sq, pst, AF.Square, (("ps", pb),), (sk,))
            else:
                self.ts("dve", self.BIG[:, dc, :], pst, scale_ap[:, dc:dc + 1], None, ALU.mult, None,
                        (("ps", pb), "pscale"), (("big", dc),))
                self.act(sq, self.BIG[:, dc, :], AF.Square, (("big", dc),), (sk,))
            self.mm(self.PS[6][:, 0:T], self.ones, sq, dc == 0, dc == DC - 1, (sk, "ones"), (("ps", 6),))
        return ev

    def phase_ffn(self, l, hf, v, src, dst, NTOK):
        c = self.c
        DC, FC = c.DC, c.FC
        T = min(512, NTOK)
        k = 2 * hf
        self.barrier()
        self.alloc_common(T)
        Wg, Wu, Wd = self.wfull[("fg", l, hf)], self.wfull[("fu", l, hf)], self.wfull[("fd", l, hf)]
        for t0 in range(0, NTOK, T):
            self.premod(src, t0, T, l, v, k)

            def ev_gu(fc, pbs):
                sg, sk = self.scr()
                self.act(sg, self.PS[pbs[0]][:, 0:T], AF.Silu, (("ps", pbs[0]),), (sk,))
                self.tt("dve", self.HB[:, fc, :], sg, self.PS[pbs[1]][:, 0:T], ALU.mult,
                        (sk, ("ps", pbs[1])), (("hb", fc),))
            self.pf(lambda ob: [(Wg, ob * P, ("fg", l, hf)), (Wu, ob * P, ("fu", l, hf))],
                    lambda ob: [(kc, kc) for kc in range(DC)],
                    lambda i: (self.XN[:, i, :], ("xn", i)), FC, T, ev_gu)
            self.pf(lambda ob: [(Wd, ob * P, ("fd", l, hf))],
                    lambda ob: [(kc, kc) for kc in range(FC)],
                    lambda i: (self.HB[:, i, :], ("hb", i)), DC, T, self.y_evac(T, DC))
            self.post_add(src, dst, t0, T, l, v, k)
            self.pump(PUMP)

    def phase_mix_out(self, l, v, actsrc, a0, src, dst, NTOK, wspec, scale_ap_loader=None, tokmajor=False):
        c = self.c
        DC = c.DC
        T = min(512, NTOK)
        self.barrier()
        self.alloc_common(T, need_h=False)
        sc = None
        if scale_ap_loader is not None:
            sc = self.alloc(DC)
            self.ld(sc, scale_ap_loader, (), ("pscale",), "ld_psc")
        a3 = None if tokmajor else actsrc.rearrange("(c p) t -> p c t", p=P)
        NTC_ = T // P
        for t0 in range(0, NTOK, T):
            q = max(1, DC // 4)
            if tokmajor and _os.environ.get("MK_MOSKIP"):
                for dc in range(DC):
                    self.op("dve", lambda e, a=self.XN[:, dc, :]: e.memset(a, 0.0), (), [("xn", dc, f0) for f0 in range(0, NTC_, 2)])
            elif tokmajor:
                stg = self.BIG.rearrange("p c t -> p (c t)").bitcast(BF16)
                for tc in range(NTC_):
                    self.ld(stg[:, tc * c.D:(tc + 1) * c.D], actsrc[a0 + t0 + tc * P:a0 + t0 + (tc + 1) * P, :],
                            (("act", id(actsrc)),), [("big", d) for d in range(DC)], "ld_big%d" % tc)
                for dc in range(DC):
                    pb = dc % 6
                    PTy = self.PS[pb][:, 0:NTC_ * 128].rearrange("p (f n) -> p f n", n=128)
                    for tc in range(NTC_):
                        self.mm(PTy[:, tc, :], stg[:, tc * c.D + dc * P:tc * c.D + (dc + 1) * P], self.ident, True, True,
                                [("big", d) for d in range(DC)] + ["ident"], (("ps", pb),))
                    xv = self.XN[:, dc, :].rearrange("p (f n) -> p f n", n=128)
                    for f0 in range(0, NTC_, 2):
                        self.cp("dve", xv[:, f0:f0 + 2, :], PTy[:, f0:f0 + 2, :],
                                (("ps", pb),), (("xn", dc, f0),))
            else:
                for i in range(0, DC, q):
                    self.ld(self.XN[:, i:i + q, :], a3[:, i:i + q, a0 + t0:a0 + t0 + T], (("act", id(actsrc)),),
                            [("xn", d) for d in range(i, i + q)], "ld_xn%d" % (i // q))
            if tokmajor:
                self._xn_keys = {i: [("xn", i, f0) for f0 in range(0, NTC_, 2)] for i in range(DC)}
            else:
                self._xn_keys = None
            mo_ = int(_os.environ.get("MK_MO", "0"))
            if not (mo_ & 2):
                self.pf(wspec[0], wspec[1], lambda i: (self.XN[:, i, :], ("xn", i)), DC, T, self.y_evac(T, DC, sc))
            if not (mo_ & 1):
                self.post_add(src, dst, t0, T, l, v, 1)
            self.pump(PUMP)

    def phase_ret_proj(self, l, v, src, NTOK, r0, nslots):
        c = self.c
        DC, D = c.DC, c.D
        T = min(512, NTOK)
        NTC = T // P
        j = l // 2
        self.barrier()
        self.alloc_common(T, need_h=False)
        W = self.wfull[("ri", j)]
        Wv = W.rearrange("(k p) n -> p k n", p=P)
        ropet = self.alloc(NTC * 512)
        rp4 = ropet.rearrange("p (c f n) -> p c f n", f=4, n=128)
        OUT = [self.alloc(256, BF16) for _ in range(4)]
        TF = [self.alloc(512) for _ in range(2)]
        TA = [self.alloc(256) for _ in range(4)]
        dsts = [self.QS, self.KS, self.VS, self.GS[0], self.GS[1]]
        scale = 256.0 ** -0.5
        ksp = min(DC, 8)
        oi = 0
        for t0 in range(0, NTOK, T):
            self.pump(PUMP)
            self.premod(src, t0, T, l, v, 1)
            if v == 0:
                self.ld(rp4.rearrange("p c f n -> p c (f n)"),
                        self.rope[t0 // P:t0 // P + NTC].rearrange("c p n -> p c n"), (), ("rope",), "ld_rope")
            ncb = nslots * D // 512
            for cb in range(ncb):
                slot, col0 = (cb * 512) // D, cb * 512
                base = (cb % 2) * 4
                for k0 in range(0, DC, ksp):
                    wt, wk, wsem = self.wslot()
                    w3 = wt[:, 0:ksp * 512].rearrange("p (k n) -> p k n", n=512)
                    self.ld(w3, Wv[:, k0:k0 + ksp, col0:col0 + 512], (("W", ("ri", j)),), (wk,), wsem)
                    for tc in range(NTC):
                        for ki in range(ksp):
                            self.mm(self.PS[base + tc][:, :], self.XN[:, k0 + ki, tc * P:(tc + 1) * P], w3[:, ki, :],
                                    (k0 + ki) == 0, (k0 + ki) == DC - 1, (wk, ("xn", k0 + ki)), (("ps", base + tc),))
                for tc in range(NTC):
                    pb = base + tc
                    ps = self.PS[pb][:, :]
                    o = OUT[oi % 4]
                    ok = ("out", oi % 4)
                    oi += 1
                    if slot <= 1 and v == 0:
                        tf = TF[oi % 2]
                        tfk = ("tf", oi % 2)
                        self.cp("act", tf, ps, (("ps", pb),), (tfk,))
                        t4 = tf.rearrange("p (h m two) -> p h m two", h=2, two=2)
                        o4 = o.rearrange("p (h m two) -> p h m two", h=2, two=2)
                        a, b = t4[:, :, :, 0], t4[:, :, :, 1]
                        cs_ = rp4[:, tc, 2 * slot, :].unsqueeze(1).to_broadcast([P, 2, 128])
                        sn_ = rp4[:, tc, 2 * slot + 1, :].unsqueeze(1).to_broadcast([P, 2, 128])
                        t1 = TA[0].rearrange("p (h m) -> p h m", h=2)
                        t2 = TA[1].rearrange("p (h m) -> p h m", h=2)
                        t3 = TA[2].rearrange("p (h m) -> p h m", h=2)
                        t4_ = TA[3].rearrange("p (h m) -> p h m", h=2)
                        self.tt("dve", t1, a, cs_, ALU.mult, (tfk, "rope"), (("ta", 0),))
                        self.tt("dve", t2, b, sn_, ALU.mult, (tfk, "rope"), (("ta", 1),))
                        self.tt("dve", o4[:, :, :, 0], t1, t2, ALU.subtract, (("ta", 0), ("ta", 1)), (ok + ("a",),))
                        self.tt("pool", t3, a, sn_, ALU.mult, (tfk, "rope"), (("ta", 2),))
                        self.tt("pool", t4_, b, cs_, ALU.mult, (tfk, "rope"), (("ta", 3),))
                        self.tt("pool", o4[:, :, :, 1], t3, t4_, ALU.add, (("ta", 2), ("ta", 3)), (ok + ("b",),))
                        rk = (ok + ("a",), ok + ("b",))
                    elif slot == 0:
                        self.act(o, ps, AF.Copy, (("ps", pb),), (ok + ("a",), ok + ("b",)), scale=scale)
                        rk = (ok + ("a",), ok + ("b",))
                    elif slot <= 2:
                        self.cp("act", o, ps, (("ps", pb),), (ok + ("a",), ok + ("b",)))
                        rk = (ok + ("a",), ok + ("b",))
                    else:
                        self.act(o, ps, AF.Silu, (("ps", pb),), (ok + ("a",), ok + ("b",)))
                        rk = (ok + ("a",), ok + ("b",))
                    r = r0 + t0 + tc * P
                    cc = col0 - slot * D
                    self.st(dsts[slot][r:r + P, cc:cc + 512], o, rk, (("qkv", slot),), "st_qkv")

    def ret_tables(self, j):
        c = self.c
        H = c.H
        X = self.alloc(2 * H)
        Z = self.alloc(2 * H)
        ACC = self.alloc(2 * H)
        self.LG = self.alloc(2 * H)
        self.M1 = self.alloc(H * 128).rearrange("p (h n) -> p h n", n=128)
        self.QD1 = self.alloc(H * 128).rearrange("p (h n) -> p h n", n=128)
        self.KD = [self.alloc(H) for _ in range(2)]
        self.CD = [self.alloc(H) for _ in range(2)]
        self.COEF = [self.alloc(H) for _ in range(2)]
        E = self.alloc(2 * H)
        self.ld(X.rearrange("p (o n) -> p o n", o=1), self.dlog[j:j + 1, :].partition_broadcast(P), (), ("dl",), "ld_dl")
        self.act(Z, X, AF.Exp, ("dl",), ("z",), scale=-1.0)
        self.ts("dve", ACC, Z, 1.0 / 7.0, None, ALU.mult, None, ("z",), ("acc",))
        for cf in (-1.0 / 6, 1.0 / 5, -1.0 / 4, 1.0 / 3, -1.0 / 2, 1.0):
            self.stt(ACC, ACC, cf, Z, ALU.add, ALU.mult, ("acc", "z"), ("acc",))
        self.ts("dve", self.LG, ACC, -1.0, None, ALU.mult, None, ("acc",), ("lg",))
        for d in range(2):
            lgd = self.LG[:, d * H:(d + 1) * H]
            self.act(self.KD[d], lgd, AF.Exp, ("lg", "cstt"), (("KD", d),), scale=self.pj[:, d:d + 1])
            self.act(self.CD[d], lgd, AF.Exp, ("lg",), (("CD", d),), scale=128.0)
            self.act(E[:, d * H:(d + 1) * H], lgd, AF.Exp, ("lg",), (("E", d),), scale=float(c.NT))
            a, b = (self.fl[:, 1:2], self.fl[:, 0:1]) if d == 0 else (self.fl[:, 0:1], self.fl[:, 1:2])
            self.ts("dve", self.COEF[d], E[:, d * H:(d + 1) * H], a, b, ALU.mult, ALU.add, (("E", d), "cstt"),
                    (("COEF", d),))

    def ret_dir_tables(self, d):
        H = self.c.H
        rel, msk, io = [self.relf, self.relb], [self.maskf, self.maskb], [self.iota1, self.iotab]
        self.M = [self.M1, self.M1]
        self.QD = [self.QD1, self.QD1]
        for h in range(H):
            col = self.LG[:, d * H + h:d * H + h + 1]
            self.act(self.M1[:, h, :], rel[d], AF.Exp, ("lg", "cstt"), (("M", 0), ("M", 1)), scale=col)
            self.act(self.QD1[:, h, :], io[d], AF.Exp, ("lg", "cstt"), (("QD", 0), ("QD", 1)), scale=col)
        self.tt("dve", self.M1, self.M1, msk[d].unsqueeze(1).to_broadcast([P, H, 128]), ALU.mult,
                (("M", 0), "cstt"), (("M", 0), ("M", 1)))

    def pair_exchange(self, snd, name, rkeys, wkey):
        rows, cols = snd.shape
        rpc = max(1, min(rows, (2 << 20) // (cols * 4)))
        while rows % rpc:
            rpc -= 1
        pairs = [[2 * i, 2 * i + 1] for i in range(self.c.NC // 2)]
        chunks = []
        for k in range(rows // rpc):
            rcv = self.dscr("%s_r%d_%d" % (name, k, self._xid), [2 * rpc, cols], F32)
            sk = snd[k * rpc:(k + 1) * rpc, :]
            if NOXCH:
                for rr_ in range(2):
                    self.op("pool", lambda e, o=rcv[rr_ * rpc:(rr_ + 1) * rpc, :], i=sk: e.dma_start(out=o, in_=i),
                            rkeys, (wkey,), "d", "st_fin")
            else:
                self.op("pool", lambda e, s_=sk, f_=rcv: e.collective_compute(
                    "AllGather", ALU.bypass, replica_groups=pairs, ins=[s_.opt()], outs=[f_.opt()]),
                    rkeys, (wkey,), "cc", "xch")
            chunks.append(rcv)
        self._xid += 1
        return rpc, chunks

    def xch_load(self, dst, xc, rk, row0, nrows, c0, c1, R, W, sem):
        rpc, chunks = xc
        r = row0
        while r < row0 + nrows:
            k = r // rpc
            hi = min(row0 + nrows, (k + 1) * rpc)
            self.ld(dst[r - row0:hi - row0], chunks[k][rk * rpc + r - k * rpc:rk * rpc + hi - k * rpc, c0:c1], R, W, sem)
            r = hi

    def ret_scan(self, j, d, r0, nch, full, first, init, final_dst, fin_key=None):
        c = self.c
        H, D, DC = c.H, c.D, c.DC
        S32, SBF = self.S32, self.SBF
        S4 = S32.rearrange("p (h c n) -> p h c n", h=H, c=2)
        B4 = SBF.rearrange("p (h c n) -> p h c n", h=H, c=2)
        f2 = 0
        if full:
            self._nfull = getattr(self, "_nfull", 0) + 1
            if self._nfull >= 2:
                f2 = int(_os.environ.get("MK_F2", "0"))
            f2 |= int(_os.environ.get("MK_F1", "0"))
        if init is None:
            self.op("dve", lambda e: e.memset(S32, 0.0), (), [("s32", h) for h in range(H)])
        if full:
            for h in range(H):
                self.cp("dve", B4[:, h], S4[:, h], (("s32", h),), (("sbf", h),))
        order = list(range(nch)) if d == 0 else list(range(nch - 1, -1, -1))
        for ci, ch in enumerate(order):
            sl = ci % 2
            r = r0 + ch * P
            Qt, Kt, Vt, Gt = self.CHT[sl]
            if _os.environ.get("MK_CHBAR"):
                self.barrier(keep=True)
            if full:
                self.ld(Qt, self.QS[r:r + P, :], (("qkv", 0),), (("cht", sl, 0),), "ld_q%d" % sl)
                self.ld(Gt, self.GS[d][r:r + P, :], (("qkv", 3 + d),), (("cht", sl, 3),), "ld_g%d" % sl)
            self.ld(Kt, self.KS[r:r + P, :], (("qkv", 1),), (("cht", sl, 1),), "ld_k%d" % sl)
            self.ld(Vt, self.VS[r:r + P, :], (("qkv", 2),), (("cht", sl, 2),), "ld_v%d" % sl)
            if full and not first:
                self.ld(self.YF[sl], self.YFS[r:r + P, :], ("yfs",), (("yf", sl),), "ld_yf%d" % sl)
            for h in range(H):
                hs = slice(h * 256, (h + 1) * 256)
                u = h % 2
                KH = self.KH[u]
                self.ts("dve", KH, Kt[:, hs], self.KD[d][:, h:h + 1], None, ALU.mult, None,
                        (("cht", sl, 1), ("KD", d)), (("kh", u),))
                fullc = full and not (f2 & 8)
                if fullc:
                    PT = self.PS[0 + u][:, :].rearrange("p (f n) -> p f n", n=128)
                    for f in range(2):
                        self.mm(PT[:, f, :], Qt[:, h * 256 + f * 128:h * 256 + (f + 1) * 128], self.ident, True, True,
                                (("cht", sl, 0), "ident"), (("ps", 0 + u),))
                        self.mm(PT[:, 2 + f, :], Kt[:, h * 256 + f * 128:h * 256 + (f + 1) * 128], self.ident, True, True,
                                (("cht", sl, 1), "ident"), (("ps", 0 + u),))
                    QT, KT, QTD = self.QT[u], self.KT[u], self.QTD[u]
                    self.cp("dve", QT, PT[:, 0:2, :], (("ps", 0 + u),), (("qt", u),))
                    self.cp("dve", KT, PT[:, 2:4, :], (("ps", 0 + u),), (("kt", u),))
                    self.tt("dve", QTD, PT[:, 0:2, :], self.QD[d][:, h, :].unsqueeze(1).to_broadcast([P, 2, 128]),
                            ALU.mult, (("ps", 0 + u), ("QD", d)), (("qtd", u),))
                    pss = self.PS[2 + u][:, 0:128]
                    for f in range(2):
                        self.mm(pss, KT[:, f, :], QT[:, f, :], f == 0, f == 1, (("kt", u), ("qt", u)), (("ps", 2 + u),))
                    ST = self.ST[u]
                    self.tt("dve", ST, pss, self.M[d][:, h, :], ALU.mult, (("ps", 2 + u), ("M", d)), (("st", u),))
                    pso = self.PS[4 + u][:, 0:256]
                    self.mm(pso, ST, Vt[:, hs], True, False, (("st", u), ("cht", sl, 2)), (("ps", 4 + u),))
                    for f in range(2):
                        self.mm(pso, QTD[:, f, :], B4[:, h, f, :], False, f == 1, (("qtd", u), ("sbf", h)),
                                (("ps", 4 + u),))
                psS = self.PS[6 + u][:, :].rearrange("p (c n) -> p c n", c=2)
                for f in range(2):
                    self.mm(psS[:, f, :], KH[:, f * 128:(f + 1) * 128], Vt[:, hs], True, True,
                            (("kh", u), ("cht", sl, 2)), (("ps", 6 + u),))
                self.stt(S4[:, h].rearrange("p c n -> p (c n)"), S4[:, h].rearrange("p c n -> p (c n)"),
                         self.CD[d][:, h:h + 1], self.PS[6 + u][:, :], ALU.mult, ALU.add,
                         (("s32", h), ("CD", d), ("ps", 6 + u)), (("s32", h),))
                if fullc:
                    self.cp("dve", B4[:, h], S4[:, h], (("s32", h),), (("sbf", h),))
                if fullc and not (f2 & 1):
                    st6, mv, rr, ON = self.ST6[u], self.MV[u], self.RR[u], self.ON[u]
                    self.op("dve", lambda e, o=st6, i=pso: e.bn_stats(o, i), (("ps", 4 + u),), (("st6", u),))
                    self.op("dve", lambda e, o=mv, i=st6: e.bn_aggr(o, i), (("st6", u),), (("mv", u),))
                    if _os.environ.get("MK_NOSQRT"):
                        self.op("dve", lambda e, r_=rr: e.memset(r_, 1.0), (("mv", u),), (("rr", u),))
                    else:
                        self.act(rr, mv[:, 1:2], AF.Sqrt, (("mv", u),), (("rr", u),), bias=self.epsc, scale=1.0)
                        self.op("dve", lambda e, r_=rr: e.reciprocal(r_, r_), (("rr", u),), (("rr", u),))
                    self.ts("dve", ON, pso, mv[:, 0:1], rr, ALU.subtract, ALU.mult,
                            (("ps", 4 + u), ("mv", u), ("rr", u)), (("on", u),))
                    GF = self.GF[u]
                    self.cp("dve", GF, Gt[:, hs], (("cht", sl, 3),), (("gf", u),))
                    if f2 & 2:
                        pass
                    elif first:
                        self.tt("dve", self.YF[sl][:, hs], ON, GF, ALU.mult, (("on", u), ("gf", u)), (("yf", sl),))
                    else:
                        ON2, YFf = self.ON2[u], self.YFF[u]
                        self.cp("dve", YFf, self.YF[sl][:, hs], (("yf", sl),), (("yff", u),))
                        self.tt("dve", ON2, ON, GF, ALU.mult, (("on", u), ("gf", u)), (("on2", u),))
                        self.tt("dve", self.YB[:, hs], ON2, YFf, ALU.add, (("on2", u), ("yff", u)), ("yb",))
            if full and first and not (f2 & 4):
                self.st(self.YFS[r:r + P, :], self.YF[sl], (("yf", sl),), ("yfs",), "st_yfs")
            if full and not first:
                self.st(self.YBS[r:r + P, :], self.YB, ("yb",), ("ybs",), "st_ybs")
        if final_dst is not None:
            self.st(final_dst, S32, [("s32", h) for h in range(H)], (fin_key,), "st_fin")

    def alloc_scan(self):
        c = self.c
        D, H, DC = c.D, c.H, c.DC
        self.CHT = []
        self.XST = []
        for _ in range(2):
            blk = self.alloc(2 * D)
            bb = blk.bitcast(BF16)
            self.CHT.append([bb[:, i * D:(i + 1) * D] for i in range(4)])
            self.XST.append(blk[:, 0:D])
        self.YF = [self.alloc(D // 2, BF16) for _ in range(2)]
        self.YB = self.alloc(D // 2, BF16)
        self.S32 = self.alloc(H * 512)
        self.SBF = self.alloc(H * 256, BF16)
        self.KH = [self.alloc(128, BF16) for _ in range(2)]
        self.QT = [self.alloc(128, BF16).rearrange("p (f n) -> p f n", n=128) for _ in range(2)]
        self.KT = [self.alloc(128, BF16).rearrange("p (f n) -> p f n", n=128) for _ in range(2)]
        self.QTD = [self.alloc(128, BF16).rearrange("p (f n) -> p f n", n=128) for _ in range(2)]
        self.ST = [self.alloc(64, BF16) for _ in range(2)]
        self.ST6 = [self.alloc(6) for _ in range(2)]
        self.MV = [self.alloc(2) for _ in range(2)]
        self.RR = [self.alloc(1) for _ in range(2)]
        self.ON = [self.alloc(256) for _ in range(2)]
        self.ON2 = [self.alloc(256) for _ in range(2)]
        self.GF = [self.alloc(256) for _ in range(2)]
        self.YFF = [self.alloc(256) for _ in range(2)]

    def phase_ret_scan(self, l):
        c = self.c
        j = l // 2
        H, NT, CTX = c.H, c.NT, c.CTX
        ctx_out = l < c.last_ret
        self.barrier()
        self.ret_tables(j)
        self.alloc_scan()
        hk = [("s32", h) for h in range(H)]
        lim = int(_os.environ.get("MK_RS", "99"))
        if lim == 0:
            return
        for d in range(2):
            self.ret_dir_tables(d)
            if lim == 1:
                return
            msk_ = int(_os.environ.get("MK_RSD", "15"))
            if msk_ & (1 << (2 * d)):
                self.ret_scan(j, d, NT, CTX // P, ctx_out and lim != 2, d == 0 or bool(_os.environ.get("MK_FF")), None, self.CST[d * P:(d + 1) * P, :], ("cstd", d))
            if lim in (2, 3):
                return
            if msk_ & (2 << (2 * d)):
                self.ret_scan(j, d, 0, c.NCH, False, d == 0, None, self.SND[d * P:(d + 1) * P, :], ("snd", d))
        if lim == 4:
            return
        xc = self.pair_exchange(self.SND, "rcv", (("snd", 0), ("snd", 1)), "rcv")
        for d in range(2):
            self.ret_dir_tables(d)
            self.ld(self.S32, self.CST[d * P:(d + 1) * P, :], (("cstd", d),), hk, "ld_s32")
            rk = 0 if d == 0 else 1
            S4 = self.S32.rearrange("p (h n) -> p h n", h=H)
            self.tt("dve", S4, S4, self.COEF[d].unsqueeze(2).to_broadcast([P, H, 512]), ALU.mult,
                    hk + [("COEF", d)], hk)
            flc = self.fl[:, 1:2] if d == 0 else self.fl[:, 0:1]
            half = H * 256
            for q in range(2):
                tq = self.XST[q]
                ck_ = [("cht", q, i_) for i_ in range(2)]
                self.xch_load(tq, xc, rk, d * P, P, q * half, (q + 1) * half, ("rcv",), ck_, "ld_xst%d" % q)
                self.stt(self.S32[:, q * half:(q + 1) * half], tq, flc, self.S32[:, q * half:(q + 1) * half],
                         ALU.mult, ALU.add, ck_ + ["cstt"] + hk, hk)
            if lim == 5:
                return
            self.ret_scan(j, d, int(_os.environ.get("MK_R0", 0)), int(_os.environ.get("MK_NCH", c.NCH)), True, d == 0, (None if _os.environ.get("MK_INIT0") else "given"), None)
            if lim == 6:
                return

    def phase_pool_in(self, l, v, src, NTOK, a0):
        c = self.c
        DC, D, GW = c.DC, c.D, c.GW
        T = min(512, NTOK)
        self.barrier()
        self.alloc_common(T, need_h=False)
        h3 = self.HXP.rearrange("(c p) t -> p c t", p=P)
        for t0 in range(0, NTOK, T):
            self.premod(src, t0, T, l, v, 1, to_f32=True)
            self.st(h3[:, :, a0 + t0:a0 + t0 + T], self.BIG, [("big", d) for d in range(DC)], ("hxp",), "st_hxp")
        two_d = (v == 0)
        if two_d:
            self.op("pool", lambda e: e.dma_start(out=self.HSND[0:D, :], in_=self.HXP[:, 0:512]), ("hxp",), ("hsnd",),
                    "d", "st_hs")
            self.op("pool", lambda e: e.dma_start(out=self.HSND[D:2 * D, :], in_=self.HXP[:, c.NT - 512:c.NT]),
                    ("hxp",), ("hsnd",), "d", "st_hs")
            hxc = self.pair_exchange(self.HSND, "hrcv", ("hsnd",), "hrcv")
        self.barrier()
        if two_d:
            R, C, HR = c.ROWS, GW, 8
        else:
            R, C, HR = 1, NTOK, 0
        RP, CP = R + 2 * HR, C + 16
        PAD = [self.alloc(RP * CP).rearrange("p (r c) -> p r c", c=CP) for _ in range(2)]
        IC = self.alloc(4 * R * C).rearrange("p (g n) -> p g n", g=4)
        PO = [self.alloc(R * C // 2, BF16) for _ in range(2)]
        ictab = self.icx if two_d else self.icc
        for g in range(4):
            self.ld(IC[:, g:g + 1, :], ictab[g:g + 1, :].partition_broadcast(P), (), ("ic",), "ld_ic")
        for i in range(2):
            self.op("dve", lambda e, a=PAD[i]: e.memset(a, 0.0), (), (("pad", i),))
        p3 = self.PSC.rearrange("(c p) t -> p c t", p=P)
        for dc in range(DC):
            g = dc // c.GDC
            nst = g + 1
            A = PAD[0]
            self.ld(A[:, HR:HR + R, 8:8 + C], h3[:, dc, a0:a0 + NTOK].rearrange("p (r c) -> p r c", c=C), ("hxp",),
                    (("pad", 0),), "ld_pad")
            if two_d:
                rpc_, chs_ = hxc
                rw = 1 * D + dc * P
                self.ld(A[:, 0:HR, 8:8 + C], chs_[rw // rpc_][0 * rpc_ + rw % rpc_:0 * rpc_ + rw % rpc_ + P, :]
                        .rearrange("p (r c) -> p r c", c=C), ("hrcv",), (("pad", 0),), "ld_pad")
                rw = 0 * D + dc * P
                self.ld(A[:, HR + R:RP, 8:8 + C], chs_[rw // rpc_][1 * rpc_ + rw % rpc_:1 * rpc_ + rw % rpc_ + P, :]
                        .rearrange("p (r c) -> p r c", c=C), ("hrcv",), (("pad", 0),), "ld_pad")
                self.ts("dve", A[:, 0:HR, 8:8 + C], A[:, 0:HR, 8:8 + C], self.fl[:, 1:2], None, ALU.mult, None,
                        (("pad", 0), "cstt"), (("pad", 0),))
                self.ts("dve", A[:, HR + R:RP, 8:8 + C], A[:, HR + R:RP, 8:8 + C], self.fl[:, 0:1], None, ALU.mult, None,
                        (("pad", 0), "cstt"), (("pad", 0),))
            cur = 0
            lo, hi = 0, CP - 1
            sh = [(1, 0), (1, 1), (2, 2), (4, 4)]
            for s in range(nst):
                sl_, sr_ = sh[s]
                nlo, nhi = lo + sl_, hi - sr_
                src_, dst_ = PAD[cur], PAD[1 - cur]
                self.tt("dve", dst_[:, :, nlo:nhi + 1], src_[:, :, nlo - sl_:nhi + 1 - sl_],
                        src_[:, :, nlo + sr_:nhi + 1 + sr_], ALU.add, (("pad", cur),), (("pad", 1 - cur),))
                lo, hi = nlo, nhi
                cur = 1 - cur
            if two_d:
                rlo, rhi = 0, RP - 1
                for s in range(nst):
                    sl_, sr_ = sh[s]
                    nlo, nhi = rlo + sl_, rhi - sr_
                    src_, dst_ = PAD[cur], PAD[1 - cur]
                    self.tt("dve", dst_[:, nlo:nhi + 1, 8:8 + C], src_[:, nlo - sl_:nhi + 1 - sl_, 8:8 + C],
                            src_[:, nlo + sr_:nhi + 1 + sr_, 8:8 + C], ALU.add, (("pad", cur),), (("pad", 1 - cur),))
                    rlo, rhi = nlo, nhi
                    cur = 1 - cur
            res = PAD[cur]
            oth = PAD[1 - cur]
            self.ld(oth[:, HR:HR + R, 8:8 + C], h3[:, dc, a0:a0 + NTOK].rearrange("p (r c) -> p r c", c=C), ("hxp",),
                    (("pad", 1 - cur),), "ld_pad2")
            po = PO[dc % 2]
            ic3 = IC[:, g, :].rearrange("p (r c) -> p r c", c=C)
            self.tt("dve", res[:, HR:HR + R, 8:8 + C], res[:, HR:HR + R, 8:8 + C], ic3, ALU.mult,
                    (("pad", cur), "ic"), (("pad", cur),))
            self.tt("dve", po.rearrange("p (r c) -> p r c", c=C), res[:, HR:HR + R, 8:8 + C],
                    oth[:, HR:HR + R, 8:8 + C], ALU.subtract, (("pad", cur), ("pad", 1 - cur)), (("po", dc % 2),))
            self.st(p3[:, dc, a0:a0 + NTOK], po, (("po", dc % 2),), ("psc",), "st_psc")
            for i in range(2):
                self.op("dve", lambda e, a=PAD[i]: e.memset(a, 0.0), (), (("pad", i),))

    def build(self):
        c = self.c
        nc = self.nc
        self.declare()
        self._xid = 0
        self._pending_bar = {}
        self._bar_prev = {}
        self._bar_start = 0
        es = ExitStack()
        self.es = es
        self.NA = 51800
        self.A = es.enter_context(nc.sbuf_tensor("arena", [P, self.NA], F32))[:, :]
        self.PS = [es.enter_context(nc.psum_tensor("ps%d" % i, [P, 512], F32)) for i in range(8)]
        self.setup_persistent()
        self.bg = []
        for l in range(c.DEPTH):
            self.prep_weight(("adn", l), self.w_in["adn"][l], "ada")
            self.prep_weight(("aup", l), self.w_in["aup"][l], "ada")
        for l_ in range(c.DEPTH):
            self.prep_layer(l_)
        self.pump()
        self.barrier(final=True)
        if self.stop_after != "prep":
            self.phase_mods()
        xs, cs = self.xT, self.ctxT
        if self.stop_after in ("prep", "mods"):
            self.barrier()
            self.op("pool", lambda e: e.dma_start(out=self.outT, in_=self.xT), (), ("outcopy",), "d", "st_x")
            self.pump()
            self.barrier(final=True)
            self.emit()
            return nc
        NT, CTX = c.NT, c.CTX
        pl_w = lambda j: (lambda ob: [(self.wfull[("pl", j)], (ob % c.GDC) * P, ("pl", j))],
                          lambda ob: [((ob // c.GDC) * c.GDC + kc, (ob // c.GDC) * c.GDC + kc) for kc in range(c.GDC)])
        ro_w = lambda j: (lambda ob: [(self.wfull[("ro", j)], ob * P, ("ro", j))],
                          lambda ob: [(kc, kc) for kc in range(c.DC)])
        done = False
        for l in range(c.DEPTH):
            ctx_in = l <= c.last_ret
            ctx_out = l < c.last_ret
            last = (l == c.DEPTH - 1)
            if l == 0 and _os.environ.get("MK_INPLACE"):
                self.barrier()
                self.op("pool", lambda e: e.dma_start(out=self.XS, in_=self.xT), (), ("xcopy",), "d", "st_x")
                xs = self.XS
            self.phase_ffn(l, 0, 0, xs, self.XS, NT)
            xs = self.XS
            if ctx_in:
                self.phase_ffn(l, 0, 1, cs, self.CS, CTX)
                cs = self.CS
            if self.stop_after == (l, 0):
                break
            if l % 2 == 0:
                j = l // 2
                self.phase_ret_proj(l, 1, cs, CTX, NT, 5 if ctx_out else 3)
                if self.stop_after == (l, "rpc"):
                    break
                self.phase_ret_proj(l, 0, xs, NT, 0, 5)
                if self.stop_after == (l, "rp"):
                    break
                self.phase_ret_scan(l)
                if self.stop_after == (l, "rs"):
                    break
                self.phase_mix_out(l, 0, self.YBS, 0, xs, self.XS, NT, ro_w(j), None, True)
                if self.stop_after == (l, "mo"):
                    break
                if ctx_out:
                    self.phase_mix_out(l, 1, self.YBS, NT, cs, self.CS, CTX, ro_w(j), None, True)
            else:
                j = l // 2
                self.phase_pool_in(l, 0, xs, NT, 0)
                if self.stop_after == (l, "pi"):
                    break
                self.phase_mix_out(l, 0, self.PSC, 0, xs, self.XS, NT, pl_w(j), self.pscale[j])
                if self.stop_after == (l, "pm"):
                    break
                if ctx_out:
                    self.phase_pool_in(l, 1, cs, CTX, NT)
                    self.phase_mix_out(l, 1, self.PSC, NT, cs, self.CS, CTX, pl_w(j), self.pscale[j])
            if self.stop_after == (l, 1):
                break
            self.phase_ffn(l, 1, 0, xs, self.outT if last else self.XS, NT)
            if ctx_out:
                self.phase_ffn(l, 1, 1, cs, self.CS, CTX)
            if self.stop_after == (l, 2):
                break
        else:
            done = True
        if not done:
            self.barrier()
            self.op("pool", lambda e: e.dma_start(out=self.outT, in_=self.XS), (), ("outcopy",), "d", "st_x")
        self.pump()
        self.barrier(final=True)
        self.emit()
        return nc

    def emit(self):
        nc = self.nc
        p = self.p
        for s in ("sp",):
            self.op(s, None, (), ())
        p.resolve()
        es = self.es
        sems = {}
        nsem = 0
        for s in Prog.CST:
            for ep in range(p.cnt[s] // EPOCH + 1):
                sems[(s, ep)] = es.enter_context(nc.semaphore("c_%s_%d" % (s, ep)))
                nsem += 1
        for g in p.groups:
            sems[g] = es.enter_context(nc.semaphore("g_" + g))
            nsem += 1
        self.nsem = nsem
        block = es.enter_context(nc.Block())
        ops, waits = p.ops, p.waits

        def run(stream, eng):
            for op in ops:
                if op.stream != stream:
                    continue
                for w in waits[op.idx]:
                    if w[0] == "c":
                        pj = ops[w[1]]
                        ep, val = divmod(pj.val - 1, EPOCH)
                        eng.wait_ge(sems[(pj.stream, ep)], val + 1)
                    else:
                        eng.wait_ge(sems[w[1]], w[2])
                if op.fn is None:
                    continue
                ins = op.fn(eng)
                if op.kind == "c":
                    if op.inc:
                        ep = (op.val - 1) // EPOCH
                        ins.then_inc(sems[(op.stream, ep)], 1)
                elif op.kind == "d":
                    ins.then_inc(sems[op.sem], 16)
                else:
                    ins.then_inc(sems[op.sem])

        @block.sync
        def _(e):
            run("sp", e)

        @block.tensor
        def _(e):
            run("pe", e)

        @block.scalar
        def _(e):
            run("act", e)

        @block.vector
        def _(e):
            run("dve", e)

        @block.gpsimd
        def _(e):
            run("pool", e)
        es.close()


def host_consts(cfg, half):
    c = cfg
    NT, GW = c.NT, c.GW
    pos = half * NT + np.arange(NT)
    rows, cols = pos // GW, pos % GW
    nf = 64
    inv = (10000.0 ** (-np.arange(nf, dtype=np.float32) / nf)).astype(np.float32)
    ang = np.concatenate([rows[:, None].astype(np.float32) * inv, cols[:, None].astype(np.float32) * inv], -1)
    cos, sin = np.cos(ang).astype(np.float32), np.sin(ang).astype(np.float32)
    sc = np.float32(256.0 ** -0.5)
    rope = np.stack([cos * sc, sin * sc, cos, sin], 1).reshape(c.NCH, P, 4 * 128).astype(np.float32)

    def cnt1(n, w, off=0, total=None):
        total = n if total is None else total
        i = off + np.arange(n)
        return (np.clip(i + w // 2, 0, total) - np.clip(i - w // 2, 0, total)).astype(np.float32)
    grows = c.S // GW
    icx = np.zeros((4, NT), np.float32)
    icc = np.zeros((4, c.CTX), np.float32)
    for g, w in enumerate((2, 4, 8, 16)):
        cr = cnt1(c.ROWS, w, half * c.ROWS, grows)
        cc = cnt1(GW, w)
        icx[g] = (1.0 / (cr[:, None] * cc[None, :])).reshape(-1)
        icc[g] = 1.0 / cnt1(c.CTX, w)
    j = np.arange(P)[:, None].astype(np.float32)
    i = np.arange(P)[None, :].astype(np.float32)
    relf, relb = np.maximum(i - j, 0), np.maximum(j - i, 0)
    maskf, maskb = (i >= j).astype(np.float32), (j >= i).astype(np.float32)
    iota1 = np.broadcast_to(i + 1, (P, P))
    iotab = np.broadcast_to(128 - i, (P, P))
    ident = np.eye(P, dtype=np.float32)
    pj = np.concatenate([127 - j, j], 1)
    fl = np.zeros((P, 2), np.float32)
    fl[:, half] = 1.0
    cst = np.concatenate([relf, relb, maskf, maskb, iota1, iotab, ident, pj, fl], 1).astype(np.float32)
    return rope, icx, icc, cst


def fm(vec, DC):
    return np.ascontiguousarray(np.moveaxis(vec.reshape(vec.shape[:-1] + (DC, P)), -1, 0))


def make_in_maps(cfg, inp):
    c = cfg
    NC, DC = c.NC, c.DC
    f32 = lambda a: np.asarray(a, dtype=np.float32)
    x, cvec, ctx, c_ctx = f32(inp["x"]), f32(inp["c"]), f32(inp["ctx"]), f32(inp["c_ctx"])
    wad, wau = f32(inp["w_ada_down"]), f32(inp["w_ada_up"])
    bada = fm(f32(inp["b_ada"]).reshape(c.DEPTH, 9, c.D), DC)
    bada = np.ascontiguousarray(np.moveaxis(bada, 0, 1)).reshape(c.DEPTH, P, 9 * DC)
    gain = fm(f32(inp["norm_gain"]), DC)
    gain = np.ascontiguousarray(np.moveaxis(gain, 0, 1)).reshape(c.DEPTH, P, 6 * DC)
    psc = np.ascontiguousarray(np.moveaxis(fm(f32(inp["pool_scale"]), DC), 0, 1))
    dlog = f32(inp["ret_decay_logit"]).reshape(c.NRET, 2 * c.H)
    wfg, wfu, wfd = f32(inp["w_ffn_gate"]), f32(inp["w_ffn_up"]), f32(inp["w_ffn_down"])
    wri, wro = f32(inp["w_ret_in"]), f32(inp["w_ret_out"])
    wpl = f32(inp["w_pool"]).reshape(c.NPOOL, c.D, c.D // 4)
    consts = [host_consts(c, h) for h in range(2)]
    maps = []
    for r in range(NC):
        b, half = r // 2, r % 2
        rope, icx, icc, cst = consts[half]
        cv = np.stack([fm(cvec[b], DC), fm(c_ctx, DC)], -1).reshape(P, DC * 2)
        def sl(w, ax):
            K, N = w.shape[ax], w.shape[ax + 1]
            assert ax + 2 == w.ndim
            rows = K // NC
            rc = chunk_rows(rows, N)
            v = w.reshape(w.shape[:ax] + (rows // rc, NC, rc, N))
            return np.ascontiguousarray(np.take(v, r, axis=ax + 1)).reshape(w.shape[:ax] + (rows, N))
        m = {
            "xT": np.ascontiguousarray(x[b, half * c.NT:(half + 1) * c.NT, :].T),
            "ctxT": np.ascontiguousarray(ctx[b].T),
            "cvT": np.ascontiguousarray(cv), "bada": bada, "gain": gain, "pscale": psc, "dlog": dlog,
            "rope": rope, "icx": icx, "icc": icc, "cst": cst,
            "w_adn": sl(wad, 1), "w_aup": sl(wau, 1), "w_fg": sl(wfg, 2), "w_fu": sl(wfu, 2), "w_fd": sl(wfd, 2),
            "w_ri": sl(wri, 1), "w_ro": sl(wro, 1), "w_pl": sl(wpl, 1),
        }
        maps.append(m)
    return maps


_CACHE = {}


def run(cfg, inp, stop_after=None, trace=False):
    key = (cfg.D, cfg.S, cfg.B, stop_after)
    if key not in _CACHE:
        b = Builder(cfg, stop_after)
        _CACHE[key] = (b.build(), b)
    nc, b = _CACHE[key]
    maps = make_in_maps(cfg, inp)
    res = run_bass_kernel_spmd(nc, maps, core_ids=list(range(cfg.NC)))
    out = np.empty((cfg.B, cfg.S, cfg.D), np.float32)
    for r in range(cfg.NC):
        out[r // 2, (r % 2) * cfg.NT:(r % 2 + 1) * cfg.NT, :] = res.results[r]["outT"].T
    return out


def kernel(**inputs):
    return run(Cfg(), inputs)
```
